# Optimizing a Trainium2 kernel written in Bass

```python
import math
import jax, jax.numpy as jnp
from jax import lax
import numpy as np

D_MODEL = 1024
BATCH = 1
SEQ = 16384
DEPTH = 2
DEC_BATCH = 16
DEC_SEQ = 4096
PAST_LEN = 128

GRID_W = 64
HEAD_DIM = 64
RWKV_HEADS = 8
RWKV_DIM = RWKV_HEADS * HEAD_DIM
ATTN_HEADS = 8
KV_HEADS = 2
ATTN_GROUP = ATTN_HEADS // KV_HEADS
ATTN_DIM = ATTN_HEADS * HEAD_DIM
KV_DIM = KV_HEADS * HEAD_DIM
MIX_DIM = RWKV_DIM + ATTN_DIM
DECAY_LORA = 64
ICLR_LORA = 64
GATE_LORA = 128
N_DIR = 2
CONV_W = 3
D_FF = ((8 * D_MODEL // 3 + 255) // 256) * 256
IN_SPLITS = [3 * RWKV_DIM, ATTN_DIM, KV_DIM, KV_DIM, N_DIR * DECAY_LORA, N_DIR * ICLR_LORA, GATE_LORA]
IN_COLS = sum(IN_SPLITS)
BLOCK_Q = 128
ROPE_THETA = 10000.0
NORM_EPS = 1e-6
QK_EPS = 1e-6
GN_EPS = 64e-5
DECAY_SCALE = math.exp(-0.5)

kernel_name = "hymba_rwkv7_gqa_axialrope_encoder"


def _split_points(sizes):
    return [int(s) for s in np.cumsum(sizes)[:-1]]


def rms_norm(x, g, eps):
    xf = x.astype(jnp.float32)
    y = xf * lax.rsqrt(jnp.mean(xf * xf, axis=-1, keepdims=True) + eps)
    return (y * g.astype(jnp.float32)).astype(x.dtype)


def centred_conv(x, w):
    xp = jnp.pad(x, ((0, 0), (1, 1), (0, 0)))
    return w[0] * xp[:, :-2] + w[1] * xp[:, 1:-1] + w[2] * xp[:, 2:]


def rope_1d(x, pos):
    quarter = x.shape[-1] // 2
    freq = 1.0 / (ROPE_THETA ** (jnp.arange(quarter, dtype=jnp.float32) / quarter))
    ang = pos.astype(jnp.float32)[:, None] * freq[None, :]
    cos = jnp.cos(ang)[None, :, None, :]
    sin = jnp.sin(ang)[None, :, None, :]
    x1, x2 = x[..., :quarter], x[..., quarter:]
    return jnp.concatenate([x1 * cos - x2 * sin, x2 * cos + x1 * sin], axis=-1)


def axial_rope(x, row, col):
    half = HEAD_DIM // 2
    xf = x.astype(jnp.float32)
    out = jnp.concatenate([rope_1d(xf[..., :half], row), rope_1d(xf[..., half:], col)], axis=-1)
    return out.astype(x.dtype)


def block_attention(q, k, v):
    B, T = q.shape[0], q.shape[1]
    nblk = T // BLOCK_Q
    qb = q.reshape(B, nblk, BLOCK_Q, KV_HEADS, ATTN_GROUP, HEAD_DIM).transpose(1, 0, 2, 3, 4, 5)
    kf = k.astype(jnp.float32)
    vf = v.astype(jnp.float32)
    scale = HEAD_DIM ** -0.5

    def one_block(qblk):
        s = jnp.einsum('bqhgd,bshd->bhgqs', qblk.astype(jnp.float32), kf) * scale
        p = jax.nn.softmax(s, axis=-1)
        return jnp.einsum('bhgqs,bshd->bqhgd', p, vf)

    o = lax.map(one_block, qb)
    return o.transpose(1, 0, 2, 3, 4, 5).reshape(B, T, ATTN_DIM)


def rwkv7_scan(r, w, k, v, a, b, reverse):
    B, T, H, N = r.shape
    xs = tuple(jnp.moveaxis(z, 1, 0) for z in (r, w, k, v, a, b))

    def step(S, inp):
        rt, wt, kt, vt, at, bt = inp
        sa = jnp.einsum('bhvk,bhk->bhv', S, at)
        S = S * wt[:, :, None, :] + sa[..., :, None] * bt[..., None, :] + vt[..., :, None] * kt[..., None, :]
        y = jnp.einsum('bhvk,bhk->bhv', S, rt)
        return S, y

    S0 = jnp.zeros((B, H, N, N), jnp.float32)
    _, ys = lax.scan(step, S0, xs, reverse=reverse)
    return jnp.moveaxis(ys, 0, 1)


def hybrid_layer(x, c, ada_w, ada_b, norm_mix_g, norm_ffn_g, w_in, conv_w, decay_w0, decay_up,
                 iclr_a0, iclr_up, gate_up, k_k, k_a, r_k, ln_x_g, ln_x_b, q_norm_g, k_norm_g,
                 w_out, w_ffn_in, w_ffn_out):
    B, T, _ = x.shape
    rows = T // GRID_W
    row = jnp.repeat(jnp.arange(rows, dtype=jnp.int32), GRID_W, total_repeat_length=T)
    col = jnp.tile(jnp.arange(GRID_W, dtype=jnp.int32), rows)
    f32 = jnp.float32

    mod = (jax.nn.silu(c) @ ada_w + ada_b)[:, None, :]
    shift_m, scale_m, gate_m, shift_f, scale_f, gate_f = jnp.split(mod, 6, axis=-1)

    h = rms_norm(x, norm_mix_g, NORM_EPS) * (1.0 + scale_m) + shift_m
    proj = h @ w_in
    rkv, q, ka, va, xw, xa, xg = jnp.split(proj, _split_points(IN_SPLITS), axis=-1)

    r, kr, vr = jnp.split(centred_conv(rkv, conv_w), 3, axis=-1)
    xw = xw.reshape(B, T, N_DIR, DECAY_LORA)
    xa = xa.reshape(B, T, N_DIR, ICLR_LORA)
    decay_logit = decay_w0 + jnp.einsum('btdr,drc->btdc', jnp.tanh(xw), decay_up)
    w = jnp.exp(-DECAY_SCALE * jax.nn.sigmoid(decay_logit.astype(f32)))
    iclr = jax.nn.sigmoid((iclr_a0 + jnp.einsum('btdr,drc->btdc', xa, iclr_up)).astype(f32))
    g = (jax.nn.sigmoid(xg) @ gate_up).astype(f32)

    rf = r.astype(f32)
    kf = kr.astype(f32)
    vf = vr.astype(f32)
    kkh = (kf * k_k.astype(f32)).reshape(B, T, RWKV_HEADS, HEAD_DIM)
    kk = kkh * lax.rsqrt(jnp.sum(kkh * kkh, axis=-1, keepdims=True) + 1e-12)
    k_dir = kf[:, :, None, :] * (1.0 + (iclr - 1.0) * k_a.astype(f32))

    heads = lambda z: z.reshape(B, T, RWKV_HEADS, HEAD_DIM)
    rh, vh = heads(rf), heads(vf)
    y_fwd = rwkv7_scan(rh, heads(w[:, :, 0]), heads(k_dir[:, :, 0]), vh, -kk,
                       kk * heads(iclr[:, :, 0]), reverse=False)
    y_bwd = rwkv7_scan(rh, heads(w[:, :, 1]), heads(k_dir[:, :, 1]), vh, -kk,
                       kk * heads(iclr[:, :, 1]), reverse=True)
    y = y_fwd + y_bwd
    mu = jnp.mean(y, axis=-1, keepdims=True)
    var = jnp.mean(jnp.square(y - mu), axis=-1, keepdims=True)
    yn = ((y - mu) * lax.rsqrt(var + GN_EPS)).reshape(B, T, RWKV_DIM)
    yn = yn * ln_x_g.astype(f32) + ln_x_b.astype(f32)
    k_bonus = heads(0.5 * (k_dir[:, :, 0] + k_dir[:, :, 1]))
    bonus = jnp.sum(rh * k_bonus * r_k.astype(f32), axis=-1, keepdims=True) * vh
    rwkv_out = ((yn + bonus.reshape(B, T, RWKV_DIM)) * g).astype(x.dtype)

    qh = rms_norm(q.reshape(B, T, ATTN_HEADS, HEAD_DIM), q_norm_g, QK_EPS)
    kh = rms_norm(ka.reshape(B, T, KV_HEADS, HEAD_DIM), k_norm_g, QK_EPS)
    vh_a = va.reshape(B, T, KV_HEADS, HEAD_DIM)
    qh = axial_rope(qh, row, col)
    kh = axial_rope(kh, row, col)
    attn_out = block_attention(qh, kh, vh_a).astype(x.dtype)

    mix = jnp.concatenate([rwkv_out, attn_out], axis=-1) @ w_out
    x = x + gate_m * mix

    h2 = rms_norm(x, norm_ffn_g, NORM_EPS) * (1.0 + scale_f) + shift_f
    gt, up = jnp.split(h2 @ w_ffn_in, 2, axis=-1)
    x = x + gate_f * ((jax.nn.silu(gt) * up) @ w_ffn_out)
    return x


def setup_inputs(seed: int = 0) -> dict:
    key = jax.random.key(seed)
    ks = jax.random.split(key, 32)
    nrm = lambda i, shape: jax.random.normal(ks[i], shape, jnp.float32)
    centre = jnp.array([0.0, 1.0, 0.0], jnp.float32)[None, :, None]
    return {
        "x_prompt": nrm(0, (BATCH, SEQ, D_MODEL)),
        "x_sample": nrm(1, (DEC_BATCH, DEC_SEQ, D_MODEL)),
        "c_prompt": nrm(2, (BATCH, D_MODEL)),
        "c_sample": nrm(3, (DEC_BATCH, D_MODEL)),
        "ada_w": nrm(4, (DEPTH, D_MODEL, 6 * D_MODEL)) * (0.5 * D_MODEL ** -0.5),
        "ada_b": nrm(5, (DEPTH, 6 * D_MODEL)) * 0.02,
        "norm_mix_g": 1.0 + 0.02 * nrm(6, (DEPTH, D_MODEL)),
        "norm_ffn_g": 1.0 + 0.02 * nrm(7, (DEPTH, D_MODEL)),
        "w_in": nrm(8, (DEPTH, D_MODEL, IN_COLS)) * D_MODEL ** -0.5,
        "conv_w": centre + 0.2 * nrm(9, (DEPTH, CONV_W, 3 * RWKV_DIM)),
        "decay_w0": 0.5 * nrm(10, (DEPTH, N_DIR, RWKV_DIM)),
        "decay_up": nrm(11, (DEPTH, N_DIR, DECAY_LORA, RWKV_DIM)) * (0.5 * DECAY_LORA ** -0.5),
        "iclr_a0": 0.5 * nrm(12, (DEPTH, N_DIR, RWKV_DIM)),
        "iclr_up": nrm(13, (DEPTH, N_DIR, ICLR_LORA, RWKV_DIM)) * (0.5 * ICLR_LORA ** -0.5),
        "gate_up": nrm(14, (DEPTH, GATE_LORA, RWKV_DIM)) * GATE_LORA ** -0.5,
        "k_k": 0.85 + 0.1 * nrm(15, (DEPTH, RWKV_DIM)),
        "k_a": 1.0 + 0.1 * nrm(16, (DEPTH, RWKV_DIM)),
        "r_k": 0.1 * nrm(17, (DEPTH, RWKV_HEADS, HEAD_DIM)),
        "ln_x_g": 1.0 + 0.02 * nrm(18, (DEPTH, RWKV_DIM)),
        "ln_x_b": 0.02 * nrm(19, (DEPTH, RWKV_DIM)),
        "q_norm_g": 1.0 + 0.02 * nrm(20, (DEPTH, HEAD_DIM)),
        "k_norm_g": 1.0 + 0.02 * nrm(21, (DEPTH, HEAD_DIM)),
        "w_out": nrm(22, (DEPTH, MIX_DIM, D_MODEL)) * MIX_DIM ** -0.5,
        "w_ffn_in": nrm(23, (DEPTH, D_MODEL, 2 * D_FF)) * D_MODEL ** -0.5,
        "w_ffn_out": nrm(24, (DEPTH, D_FF, D_MODEL)) * D_FF ** -0.5,
    }


def reference(x_prompt, x_sample, c_prompt, c_sample, ada_w, ada_b, norm_mix_g, norm_ffn_g, w_in,
              conv_w, decay_w0, decay_up, iclr_a0, iclr_up, gate_up, k_k, k_a, r_k, ln_x_g, ln_x_b,
              q_norm_g, k_norm_g, w_out, w_ffn_in, w_ffn_out):
    def run_trunk(x, c):
        for l in range(DEPTH):
            x = hybrid_layer(x, c, ada_w[l], ada_b[l], norm_mix_g[l], norm_ffn_g[l], w_in[l],
                             conv_w[l], decay_w0[l], decay_up[l], iclr_a0[l], iclr_up[l],
                             gate_up[l], k_k[l], k_a[l], r_k[l], ln_x_g[l], ln_x_b[l],
                             q_norm_g[l], k_norm_g[l], w_out[l], w_ffn_in[l], w_ffn_out[l])
        return x

    y_prompt = run_trunk(x_prompt, c_prompt)
    y_sample = run_trunk(x_sample, c_sample)
    return (y_prompt, y_sample)
```

```python
import math
from contextlib import ExitStack
import numpy as np
import ml_dtypes
import concourse.bass as bass
import concourse.mybir as mybir
from concourse.bass_utils import run_bass_kernel_spmd

F32 = mybir.dt.float32
BF16 = mybir.dt.bfloat16
AF = mybir.ActivationFunctionType
ALU = mybir.AluOpType
AX = mybir.AxisListType

D = 1024
DFF = 2816
INC = 2688
ENGS = ("pe", "act", "dve", "pool", "sp")
DEC_SCALE = math.exp(-0.5)


class Buf:
    __slots__ = ("lw", "rd")

    def __init__(self):
        self.lw = None
        self.rd = {}


class Prog:
    def __init__(self, nc, es):
        self.nc = nc
        self.ops = {e: [] for e in ENGS}
        self.cnt = {e: 0 for e in ENGS}
        self.waited = {e: {} for e in ENGS}
        self.NDS = 48
        self.ndma = 0
        self.dma_cnt = [0] * self.NDS
        self.sems = {}
        self.es = es
        self.epoch = {e: 0 for e in ENGS}
        self.EPOCH_MAX = 24000
        for e in ENGS:
            self.sems[(e, 0)] = es.enter_context(nc.semaphore("s_" + e))
        for i in range(self.NDS):
            self.sems[("d", i)] = es.enter_context(nc.semaphore("s_d%d" % i))

    def _deps(self, reads, writes):
        deps = {}
        for b in reads:
            if b.lw is not None and deps.get(b.lw[0], 0) < b.lw[1]:
                deps[b.lw[0]] = b.lw[1]
        for b in writes:
            if b.lw is not None and deps.get(b.lw[0], 0) < b.lw[1]:
                deps[b.lw[0]] = b.lw[1]
            for k, v in b.rd.items():
                if deps.get(k, 0) < v:
                    deps[k] = v
        return deps

    def _mark(self, reads, writes, key, val):
        for b in reads:
            if b.rd.get(key, 0) < val:
                b.rd[key] = val
        for b in writes:
            b.lw = (key, val)
            b.rd = {}

    def _waits(self, eng, deps, skip_self_pe=False):
        waits = []
        w = self.waited[eng]
        for k, v in deps.items():
            if skip_self_pe and k[0] == "pe":
                continue
            if w.get(k, 0) < v:
                w[k] = v
                waits.append((k, v))
        return waits

    def op(self, eng, fn, reads=(), writes=()):
        deps = self._deps(reads, writes)
        waits = self._waits(eng, deps, skip_self_pe=(eng == "pe"))
        if self.cnt[eng] >= self.EPOCH_MAX:
            self.epoch[eng] += 1
            self.cnt[eng] = 0
            self.sems[(eng, self.epoch[eng])] = self.es.enter_context(
                self.nc.semaphore("s_%s_%d" % (eng, self.epoch[eng])))
        key = (eng, self.epoch[eng])
        self.cnt[eng] += 1
        self.ops[eng].append((waits, fn, (key, 1)))
        self._mark(reads, writes, key, self.cnt[eng])

    def dma(self, q, fn, reads=(), writes=()):
        si = self.ndma % self.NDS
        self.ndma += 1
        key = ("d", si)
        deps = self._deps(reads, writes)
        if self.dma_cnt[si] > 0:
            deps[key] = max(deps.get(key, 0), self.dma_cnt[si] * 16)
        waits = self._waits(q, deps)
        self.dma_cnt[si] += 1
        self.ops[q].append((waits, fn, (key, 16)))
        self._mark(reads, writes, key, self.dma_cnt[si] * 16)

    def maybe_emit(self, limit=16000):
        n = sum(len(v) + sum(len(x[0]) for x in v) for v in self.ops.values())
        if n >= limit:
            self.emit()

    def wait_all(self, eng, bufs):
        deps = self._deps(bufs, ())
        self.ops[eng].append((self._waits(eng, deps), None, None))

    def emit(self):
        nc = self.nc
        sems = self.sems
        ops = self.ops
        with nc.Block() as block:
            def run(handle, lst):
                for waits, fn, inc in lst:
                    for k, v in waits:
                        handle.wait_ge(sems[k], v)
                    if fn is not None:
                        fn(handle).then_inc(sems[inc[0]], inc[1])

            @block.tensor
            def _(h):
                run(h, ops["pe"])

            @block.scalar
            def _(h):
                run(h, ops["act"])

            @block.vector
            def _(h):
                run(h, ops["dve"])

            @block.gpsimd
            def _(h):
                run(h, ops["pool"])

            @block.sync
            def _(h):
                run(h, ops["sp"])
        self.ops = {e: [] for e in ENGS}


class Tl:
    def __init__(self, t, b=None):
        self.t = t
        self.b = b if b is not None else Buf()

    def __getitem__(self, k):
        return self.t[k]


def host_consts(tmax):
    j = np.arange(128)[:, None]
    t = np.arange(128)[None, :]
    ident = (j == t).astype(np.float32)
    su = (j < t).astype(np.float32)
    u = (j <= t).astype(np.float32)
    sl = (j > t).astype(np.float32)
    lo = (j >= t).astype(np.float32)
    ones = np.ones((128, 128), np.float32)
    cst = np.concatenate([ident, su, su, u, u, sl, sl, lo, lo, ones], axis=1).astype(np.float32)
    pos = np.arange(tmax)
    row = (pos // 64).astype(np.float32)
    col = (pos % 64).astype(np.float32)
    freq = (1.0 / (10000.0 ** (np.arange(16, dtype=np.float32) / 16.0))).astype(np.float32)
    ar = row[:, None] * freq[None, :]
    ac = col[:, None] * freq[None, :]
    cos64 = np.concatenate([np.cos(ar), np.cos(ar), np.cos(ac), np.cos(ac)], axis=1)
    sin64 = np.concatenate([-np.sin(ar), np.sin(ar), -np.sin(ac), np.sin(ac)], axis=1)
    cosq = np.tile(cos64, (1, 8)).astype(np.float32)
    sinq = np.tile(sin64, (1, 8)).astype(np.float32)
    return cst, cosq, sinq


C_ID, C_SU2, C_U2, C_SL2, C_L2, C_ONES = 0, 128, 384, 640, 896, 1152
CW = 1280


def build(seqs, depth, debug=False):
    NS = len(seqs)
    tmax = max(seqs)
    nc = bass.Bass("TRN2", target_bir_lowering=False)
    es = ExitStack()
    P = Prog(nc, es)
    _uid = [0]
    _orig_sbuf = nc.sbuf_tensor

    def _sbuf(name, shape, dt):
        _uid[0] += 1
        return _orig_sbuf("sb%d_%s" % (_uid[0], name), shape, dt)

    def dram(name, shape, dt=F32, kind="Internal"):
        return Tl(nc.dram_tensor(name, list(shape), dt, kind=kind).ap())

    xin = [dram("x%d" % s, [seqs[s], D], kind="ExternalInput") for s in range(NS)]
    yout = [dram("y%d" % s, [seqs[s], D], kind="ExternalOutput") for s in range(NS)]
    cvec = dram("cvec", [NS, D], kind="ExternalInput")
    cst_d = dram("cst", [128, CW], kind="ExternalInput")
    cosq_d = dram("cosq", [tmax, 512], kind="ExternalInput")
    sinq_d = dram("sinq", [tmax, 512], kind="ExternalInput")
    wshapes = dict(ada_w=[depth, D, 6 * D], ada_b=[depth, 6 * D], norm_mix_g=[depth, D], norm_ffn_g=[depth, D],
                   w_in=[depth, D, INC], conv_w=[depth, 3, 1536], decay_w0=[depth, 2, 512],
                   decay_up=[depth, 2, 64, 512], iclr_a0=[depth, 2, 512], iclr_up=[depth, 2, 64, 512],
                   gate_up=[depth, 128, 512], k_k=[depth, 512], k_a=[depth, 512], r_k=[depth, 8, 64],
                   ln_x_g=[depth, 512], ln_x_b=[depth, 512], q_norm_g=[depth, 64], k_norm_g=[depth, 64],
                   w_out=[depth, D, D], w_ffn_in=[depth, D, 2 * DFF], w_ffn_out=[depth, DFF, D])
    W = {k: dram(k, v, kind="ExternalInput") for k, v in wshapes.items()}

    skind = "ExternalOutput" if debug else "Internal"
    XS = [dram("XS%d" % s, [seqs[s], D], kind=skind) for s in range(NS)]
    X1 = [dram("X1_%d" % s, [seqs[s], D], kind=skind) for s in range(NS)]
    RKV = [dram("RKV%d" % s, [seqs[s] + 2, 1536], kind=skind) for s in range(NS)]
    DL = [dram("DL%d" % s, [seqs[s], 1024], kind=skind) for s in range(NS)]
    IC = [dram("IC%d" % s, [seqs[s], 1024], kind=skind) for s in range(NS)]
    GG = [dram("GG%d" % s, [seqs[s], 512], kind=skind) for s in range(NS)]
    QT = [dram("QT%d" % s, [seqs[s] // 128, 64, 1024], BF16, kind=skind) for s in range(NS)]
    KT = [dram("KT%d" % s, [64, 2, seqs[s]], BF16, kind=skind) for s in range(NS)]
    VA = [dram("VA%d" % s, [seqs[s], 130], BF16, kind=skind) for s in range(NS)]
    AT = [dram("AT%d" % s, [seqs[s] // 128, 64, 1024], BF16, kind=skind) for s in range(NS)]
    RT = [dram("RT%d" % s, [seqs[s] // 128, 128, 512], BF16, kind=skind) for s in range(NS)]
    YD = [[dram("YD%d_%d" % (s, d), [seqs[s], 512], kind=skind) for d in range(2)] for s in range(NS)]
    BD = [[dram("BD%d_%d" % (s, d), [seqs[s], 512], kind=skind) for d in range(2)] for s in range(NS)]
    H2T = [dram("H2T%d" % s, [seqs[s] // 128, 128, 1024], BF16, kind=skind) for s in range(NS)]

    def sb(name, shape, dt=F32):
        return Tl(es.enter_context(_sbuf(name, list(shape), dt)))

    psum_all = es.enter_context(nc.psum_tensor("psall", [128, 4096], F32))
    PB = [Buf() for _ in range(8)]

    def ps(bank, c0=0, c1=512, p0=0, p1=128):
        return psum_all[p0:p1, bank * 512 + c0: bank * 512 + c1]

    def psb(bank, c0=0, c1=1024, p0=0, p1=128):
        v = psum_all[p0:p1, bank * 512:(bank + 1) * 512].bitcast(BF16)
        return v[:, c0:c1]

    cst = sb("cst", [128, CW])
    identb = sb("identb", [128, 128], BF16)
    MODD = dram("MODD", [NS, 6 * D])
    P.dma("sp", lambda h: h.dma_start(out=cst[:, :], in_=cst_d[:, :]), writes=[cst.b])
    P.op("dve", lambda h: h.tensor_copy(out=identb[:, :], in_=cst[:, 0:128]), reads=[cst.b], writes=[identb.b])
    P.emit()

    def CS(c0, w):
        return cst[:, c0:c0 + w]

    def mm(out, lhsT, rhs, start, stop, reads, writes):
        P.op("pe", lambda h: h.matmul(out, lhsT, rhs, start=start, stop=stop), reads=reads, writes=writes)

    def tr(out, in_, reads, writes):
        P.op("pe", lambda h: h.transpose(out, in_, identb[:, :]), reads=list(reads) + [identb.b], writes=writes)

    def act(out, in_, func, reads, writes, **kw):
        P.op("act", lambda h: h.activation(out=out, in_=in_, func=func, **kw), reads=reads, writes=writes)

    def tt(eng, out, in0, in1, op, reads, writes):
        P.op(eng, lambda h: h.tensor_tensor(out=out, in0=in0, in1=in1, op=op), reads=reads, writes=writes)

    def ts(eng, out, in0, s1, s2, op0, op1, reads, writes):
        if op1 is None:
            P.op(eng, lambda h: h.tensor_scalar(out=out, in0=in0, scalar1=s1, scalar2=None, op0=op0),
                 reads=reads, writes=writes)
        else:
            P.op(eng, lambda h: h.tensor_scalar(out=out, in0=in0, scalar1=s1, scalar2=s2, op0=op0, op1=op1),
                 reads=reads, writes=writes)

    def stt(eng, out, in0, scalar, in1, op0, op1, reads, writes):
        P.op(eng, lambda h: h.scalar_tensor_tensor(out=out, in0=in0, scalar=scalar, in1=in1, op0=op0, op1=op1),
             reads=reads, writes=writes)

    def cp(eng, out, in_, reads, writes):
        if eng == "act":
            act(out, in_, AF.Copy, reads, writes)
        else:
            P.op(eng, lambda h: h.tensor_copy(out=out, in_=in_), reads=reads, writes=writes)

    def red(out, in_, reads, writes):
        P.op("dve", lambda h: h.tensor_reduce(out=out, in_=in_, axis=AX.X, op=ALU.add), reads=reads, writes=writes)

    def ld(out, in_, reads, writes, q="sp", **kw):
        P.dma(q, lambda h: h.dma_start(out=out, in_=in_, **kw), reads=reads, writes=writes)

    def st(out, in_, reads, writes, **kw):
        P.dma("pool", lambda h: h.dma_start(out=out, in_=in_, **kw), reads=reads, writes=writes)

    def bc(ap2, nh, d=64):
        return ap2.unsqueeze(2).to_broadcast([ap2.shape[0], nh, d])

    def h3(ap2, d=64):
        return ap2.rearrange("p (h d) -> p h d", d=d)

    def rstd_from(out, ssum, scale, eps, reads_b):
        ts("dve", out, ssum, scale, eps, ALU.mult, ALU.add, [reads_b], [reads_b])
        act(out, out, AF.Sqrt, [reads_b], [reads_b])
        P.op("dve", lambda h: h.reciprocal(out=out, in_=out), [reads_b], [reads_b])

    for l in range(depth):
        with ExitStack() as ph:
            def psb_(name, shape, dt=F32):
                return Tl(ph.enter_context(_sbuf(name, list(shape), dt)))
            cT = psb_("cT", [128, 8, NS])
            modt = psb_("modt", [NS, 6 * D])
            aw = [psb_("aw%d" % i, [128, 8, 512]) for i in range(2)]
            abt = psb_("abt", [NS, 6 * D])
            ngt = psb_("ngt", [NS, 2, D])
            crow = psb_("crow", [NS, D])
            ld(crow[:, :], cvec.t[:, :], [], [crow.b])
            act(crow[:, :], crow[:, :], AF.Silu, [crow.b], [crow.b])
            for k in range(8):
                P.op("pe", lambda h, k=k: h.transpose(ps(2, k * NS, (k + 1) * NS), crow[:, k * 128:(k + 1) * 128], cst[0:NS, 0:NS]),
                     reads=[crow.b, cst.b], writes=[PB[2]])
            cp("act", cT[:, :, :].rearrange("p k s -> p (k s)"), ps(2, 0, 8 * NS), [PB[2]], [cT.b])
            ld(abt[:, :], W["ada_b"].t[l:l + 1, :].partition_broadcast(NS) if NS > 1 else W["ada_b"].t[l:l + 1, :],
               [], [abt.b])
            ld(ngt[:, 0, :], W["norm_mix_g"].t[l:l + 1, :].partition_broadcast(NS) if NS > 1 else W["norm_mix_g"].t[l:l + 1, :], [], [ngt.b])
            ld(ngt[:, 1, :], W["norm_ffn_g"].t[l:l + 1, :].partition_broadcast(NS) if NS > 1 else W["norm_ffn_g"].t[l:l + 1, :], [], [ngt.b])
            awv = W["ada_w"].t[l].rearrange("(k p) n -> p k n", p=128)
            for j in range(12):
                a = aw[j % 2]
                ld(a[:, :, :], awv[:, :, j * 512:(j + 1) * 512], [], [a.b])
                for k in range(8):
                    mm(ps(j % 2, p1=NS), cT[:, k, :], a[:, k, :], k == 0, k == 7, [cT.b, a.b], [PB[j % 2]])
                tt("dve", modt[:, j * 512:(j + 1) * 512], ps(j % 2, p1=NS), abt[:, j * 512:(j + 1) * 512], ALU.add,
                   [PB[j % 2], abt.b], [modt.b])
            stt("dve", modt[:, D:2 * D], modt[:, D:2 * D], 1.0, ngt[:, 0, :], ALU.add, ALU.mult, [modt.b, ngt.b], [modt.b])
            stt("dve", modt[:, 4 * D:5 * D], modt[:, 4 * D:5 * D], 1.0, ngt[:, 1, :], ALU.add, ALU.mult, [modt.b, ngt.b], [modt.b])
            st(MODD.t[:, :], modt[:, :], [modt.b], [MODD.b])
            P.emit()

        def bcast_table(dst, s, idx):
            ld(dst[:, :], MODD.t[s:s + 1, idx * D:(idx + 1) * D].partition_broadcast(128), [MODD.b], [dst.b])

        def load_bf16(dst_view_fn, src_view_fn, nrows_k, ncols, stage, reads_extra=()):
            i = 0
            for k in range(nrows_k):
                for c0 in range(0, ncols, 2048):
                    w = min(2048, ncols - c0)
                    s_ = stage[i % 2]
                    i += 1
                    dsts = dst_view_fn(k, c0, w)
                    ld(s_[0:dsts[2], 0:w], src_view_fn(k, c0, w), [], [s_.b])
                    cp("dve" if i % 2 else "act", dsts[0], s_[0:dsts[2], 0:w], [s_.b], [dsts[1]])

        def bload(ph, name, shape, src, dt=F32):
            t_ = Tl(ph.enter_context(_sbuf(name, list(shape), dt)))
            ld(t_.t[tuple(slice(None) for _ in shape)], src, [], [t_.b])
            return t_

        with ExitStack() as ph:
            def T_(name, shape, dt=F32):
                return Tl(ph.enter_context(_sbuf(name, list(shape), dt)))
            win = T_("win", [128, 8, INC], BF16)
            stage = [T_("stg%d" % i, [128, 2048]) for i in range(2)]
            wv = W["w_in"].t[l].rearrange("(k p) n -> p k n", p=128)
            load_bf16(lambda k, c0, w: (win[:, k, c0:c0 + w], win.b, 128), lambda k, c0, w: wv[:, k, c0:c0 + w], 8, INC, stage)
            dup = T_("dup", [128, 512], BF16)
            iup = T_("iup", [128, 512], BF16)
            gup = T_("gup", [128, 512], BF16)
            for (dst, src) in ((dup, W["decay_up"].t[l].rearrange("d r c -> (d r) c")),
                               (iup, W["iclr_up"].t[l].rearrange("d r c -> (d r) c")),
                               (gup, W["gate_up"].t[l])):
                s_ = stage[0]
                ld(s_[:, 0:512], src, [], [s_.b])
                cp("dve", dst[:, :], s_[:, 0:512], [s_.b], [dst.b])
            w0t = bload(ph, "w0t", [128, 1024], W["decay_w0"].t[l:l + 1].rearrange("o d c -> o (d c)").partition_broadcast(128))
            a0t = bload(ph, "a0t", [128, 1024], W["iclr_a0"].t[l:l + 1].rearrange("o d c -> o (d c)").partition_broadcast(128))
            gq = bload(ph, "gq", [128, 64], W["q_norm_g"].t[l:l + 1, :].partition_broadcast(128))
            gk = bload(ph, "gk", [128, 64], W["k_norm_g"].t[l:l + 1, :].partition_broadcast(128))
            gainA = T_("gainA", [128, D])
            shiftA = T_("shiftA", [128, D])
            gqk10 = T_("gqk10", [128, 10, 64])
            for hh in range(10):
                g_ = gq if hh < 8 else gk
                cp("dve", gqk10[:, hh, :], g_[:, :], [g_.b], [gqk10.b])
            zt = T_("zt", [1, 1536])
            P.op("dve", lambda h: h.memset(zt[:, :], 0.0), [], [zt.b])
            xt = [T_("xt%d" % i, [128, D]) for i in range(2)]
            junk = T_("junk", [128, D])
            st1 = T_("st1", [128, 32])
            hb = T_("hb", [128, D], BF16)
            hT = T_("hT", [128, 8, 128], BF16)
            rkvt = T_("rkvt", [128, 1536])
            lor = T_("lor", [128, 3, 128], BF16)
            dlt = T_("dlt", [128, 1024])
            ict = T_("ict", [128, 1024])
            ggt = T_("ggt", [128, 512])
            cosb = T_("cosb", [128, 512])
            sinb = T_("sinb", [128, 512])
            qn = T_("qn", [128, 640])
            q1 = T_("q1", [128, 640])
            q2 = T_("q2", [128, 640])
            qrb = T_("qrb", [128, 640], BF16)
            qtt = T_("qtt", [64, 10, 128], BF16)
            vat = T_("vat", [128, 2, 65], BF16)
            P.op("dve", lambda h: h.memset(vat[:, :, :], 1.0), [], [vat.b])
            for s in range(NS):
                T = seqs[s]
                src = xin[s] if l == 0 else XS[s]
                bcast_table(gainA, s, 1)
                bcast_table(shiftA, s, 0)
                st(RKV[s].t[0:1, :], zt[:, :], [zt.b], [RKV[s].b])
                st(RKV[s].t[T + 1:T + 2, :], zt[:, :], [zt.b], [RKV[s].b])
                for i in range(T // 128):
                    t0 = i * 128
                    x_ = xt[i % 2]
                    ld(x_[:, :], src.t[t0:t0 + 128, :], [src.b], [x_.b])
                    ld(cosb[:, :], cosq_d.t[t0:t0 + 128, :], [], [cosb.b])
                    ld(sinb[:, :], sinq_d.t[t0:t0 + 128, :], [], [sinb.b])
                    P.op("dve", lambda h: h.memset(st1[:, 0:1], 0.0), [], [st1.b])
                    act(junk[:, :], x_[:, :], AF.Square, [x_.b], [junk.b, st1.b], accum_out=st1[:, 0:1])
                    rstd_from(st1[:, 0:1], st1[:, 0:1], 1.0 / D, 1e-6, st1.b)
                    stt("dve", junk[:, :], x_[:, :], st1[:, 0:1], gainA[:, :], ALU.mult, ALU.mult,
                        [x_.b, st1.b, gainA.b], [junk.b])
                    tt("dve", hb[:, :], junk[:, :], shiftA[:, :], ALU.add, [junk.b, shiftA.b], [hb.b])
                    for k in range(8):
                        tr(psb(7, k * 128, (k + 1) * 128), hb[:, k * 128:(k + 1) * 128], [hb.b], [PB[7]])
                    cp("act", hT[:, :, :].rearrange("p k t -> p (k t)"), psb(7), [PB[7]], [hT.b])
                    for g, (c0, w) in enumerate(((0, 512), (512, 512), (1024, 512), (1536, 512), (2048, 256))):
                        for k in range(8):
                            mm(ps(g, 0, w), hT[:, k, :], win[:, k, c0:c0 + w], k == 0, k == 7, [hT.b, win.b], [PB[g]])
                    for g in range(3):
                        for k in range(8):
                            mm(ps(5, g * 128, (g + 1) * 128), win[:, k, 2304 + g * 128:2304 + (g + 1) * 128], hT[:, k, :],
                               k == 0, k == 7, [hT.b, win.b], [PB[5]])
                    cp("act", rkvt[:, :], psum_all[:, 0:1536], [PB[0], PB[1], PB[2]], [rkvt.b])
                    st(RKV[s].t[1 + t0:1 + t0 + 128, :], rkvt[:, :], [rkvt.b], [RKV[s].b])
                    act(lor[:, 0, :], ps(5, 0, 128), AF.Tanh, [PB[5]], [lor.b])
                    act(lor[:, 1, :], ps(5, 128, 256), AF.Copy, [PB[5]], [lor.b])
                    act(lor[:, 2, :], ps(5, 256, 384), AF.Sigmoid, [PB[5]], [lor.b])
                    for d_ in range(2):
                        mm(ps(6 + d_), lor[64 * d_:64 * d_ + 64, 0, :], dup[64 * d_:64 * d_ + 64, :], True, True,
                           [lor.b, dup.b], [PB[6 + d_]])
                    tt("dve", dlt[:, :], psum_all[:, 6 * 512:8 * 512], w0t[:, :], ALU.add, [PB[6], PB[7], w0t.b], [dlt.b])
                    act(dlt[:, :], dlt[:, :], AF.Sigmoid, [dlt.b], [dlt.b])
                    ts("dve", dlt[:, :], dlt[:, :], -DEC_SCALE, None, ALU.mult, None, [dlt.b], [dlt.b])
                    st(DL[s].t[t0:t0 + 128, :], dlt[:, :], [dlt.b], [DL[s].b])
                    for d_ in range(2):
                        mm(ps(d_), lor[64 * d_:64 * d_ + 64, 1, :], iup[64 * d_:64 * d_ + 64, :], True, True,
                           [lor.b, iup.b], [PB[d_]])
                    mm(ps(2), lor[:, 2, :], gup[:, :], True, True, [lor.b, gup.b], [PB[2]])
                    tt("dve", ict[:, :], psum_all[:, 0:1024], a0t[:, :], ALU.add, [PB[0], PB[1], a0t.b], [ict.b])
                    act(ict[:, :], ict[:, :], AF.Sigmoid, [ict.b], [ict.b])
                    st(IC[s].t[t0:t0 + 128, :], ict[:, :], [ict.b], [IC[s].b])
                    cp("act", ggt[:, :], ps(2), [PB[2]], [ggt.b])
                    st(GG[s].t[t0:t0 + 128, :], ggt[:, :], [ggt.b], [GG[s].b])
                    cp("act", qn[:, 0:512], ps(3), [PB[3]], [qn.b])
                    cp("act", qn[:, 512:640], ps(4, 0, 128), [PB[4]], [qn.b])
                    act(q1[:, :], qn[:, :], AF.Square, [qn.b], [q1.b])
                    red(st1[:, 8:18], q1[:, :].rearrange("p (h d) -> p h d", d=64), [q1.b], [st1.b])
                    rstd_from(st1[:, 8:18], st1[:, 8:18], 1.0 / 64, 1e-6, st1.b)
                    tt("dve", h3(qn[:, :]), h3(qn[:, :]), bc(st1[:, 8:18], 10), ALU.mult, [qn.b, st1.b], [qn.b])
                    tt("dve", h3(qn[:, :]), h3(qn[:, :]), gqk10[:, :, :],
                       ALU.mult, [qn.b, gqk10.b], [qn.b])
                    for (a0, a1, c0) in ((0, 512, 0), (512, 640, 0)):
                        w_ = a1 - a0
                        tt("dve", q1[:, a0:a1], qn[:, a0:a1], cosb[:, c0:c0 + w_], ALU.mult, [qn.b, cosb.b], [q1.b])
                        v5 = lambda tl_, a, b: tl_[:, a:b].rearrange("p (h f x j) -> p h f x j", f=2, x=2, j=16)
                        tt("pool", v5(q2, a0, a1)[:, :, :, 0, :], v5(qn, a0, a1)[:, :, :, 1, :], v5(sinb, c0, c0 + w_)[:, :, :, 0, :],
                           ALU.mult, [qn.b, sinb.b], [q2.b])
                        tt("pool", v5(q2, a0, a1)[:, :, :, 1, :], v5(qn, a0, a1)[:, :, :, 0, :], v5(sinb, c0, c0 + w_)[:, :, :, 1, :],
                           ALU.mult, [qn.b, sinb.b], [q2.b])
                    tt("dve", qrb[:, :], q1[:, :], q2[:, :], ALU.add, [q1.b, q2.b], [qrb.b])
                    for hh in range(10):
                        tr(psb(5 + hh // 8, (hh % 8) * 128, (hh % 8 + 1) * 128, 0, 64), qrb[:, hh * 64:(hh + 1) * 64], [qrb.b],
                           [PB[5 + hh // 8]])
                    cp("act", qtt[:, 0:8, :].rearrange("p h t -> p (h t)"), psb(5, 0, 1024, 0, 64), [PB[5]], [qtt.b])
                    cp("act", qtt[:, 8:10, :].rearrange("p h t -> p (h t)"), psb(6, 0, 256, 0, 64), [PB[6]], [qtt.b])
                    st(QT[s].t[i], qtt[:, 0:8, :].rearrange("p h t -> p (h t)"), [qtt.b], [QT[s].b])
                    st(KT[s].t[:, :, t0:t0 + 128], qtt[:, 8:10, :], [qtt.b], [KT[s].b])
                    cp("dve", vat[:, :, 0:64], ps(4, 128, 256).rearrange("p (h d) -> p h d", d=64), [PB[4]], [vat.b])
                    st(VA[s].t[t0:t0 + 128, :], vat[:, :, :].rearrange("p h d -> p (h d)"), [vat.b], [VA[s].b])
                    P.maybe_emit()
            P.emit()

        for s in range(NS):
            T = seqs[s]
            NT = T // 128
            with ExitStack() as ph:
                def T_(name, shape, dt=F32):
                    return Tl(ph.enter_context(_sbuf(name, list(shape), dt)))
                cwt = bload(ph, "cwt", [128, 3 * 1536], W["conv_w"].t[l:l + 1].rearrange("o j c -> o (j c)").partition_broadcast(128))
                kkt = bload(ph, "kkt", [128, 512], W["k_k"].t[l:l + 1, :].partition_broadcast(128))
                kat = bload(ph, "kat", [128, 512], W["k_a"].t[l:l + 1, :].partition_broadcast(128))
                rkt = bload(ph, "rkt", [128, 512], W["r_k"].t[l:l + 1].rearrange("o h d -> o (h d)").partition_broadcast(128))
                D2 = []
                pv = T_("pv", [128, 1536])
                nx = T_("nx", [128, 1536])
                for d_ in range(2):
                    dd = {}
                    for nm, shp, dt in (("rkv", [128, 1536], F32),
                                        ("lw", [128, 512], F32), ("ic", [128, 512], F32), ("kk", [128, 512], F32),
                                        ("kd", [128, 512], F32), ("bb", [128, 512], F32), ("e1", [128, 512], F32),
                                        ("e2", [128, 512], F32), ("e3", [128, 512], F32), ("e4", [128, 512], F32),
                                        ("sm", [128, 32], F32), ("gl", [64, 8], F32),
                                        ("tm", [128, 4, 512], BF16), ("bh", [128, 512], BF16), ("kh", [128, 512], BF16),
                                        ("vb", [128, 512], BF16), ("xT", [64, 8, 4, 128], BF16),
                                        ("G", [128, 8, 3, 128], BF16), ("Wm", [128, 8, 3, 128], BF16),
                                        ("rhs", [128, 512], BF16), ("ub", [128, 512], BF16), ("yo", [128, 512], F32),
                                        ("bo", [128, 512], F32), ("ST", [64, 8, 64], F32), ("STb", [64, 8, 64], BF16)):
                        dd[nm] = T_("%s%d" % (nm, d_), shp, dt)
                    D2.append(dd)
                    P.op("dve", lambda h, dd=dd: h.memset(dd["ST"][:, :, :], 0.0), [], [dd["ST"].b])
                    P.op("dve", lambda h, dd=dd: h.memset(dd["STb"][:, :, :], 0.0), [], [dd["STb"].b])
                for n in range(NT):
                    for d_ in range(2):
                        i = n if d_ == 0 else NT - 1 - n
                        t0 = i * 128
                        A = D2[d_]
                        cMsu = CS(C_SU2, 256) if d_ == 0 else CS(C_SL2, 256)
                        cMu = CS(C_U2, 256) if d_ == 0 else CS(C_L2, 256)
                        cMsl = CS(C_SL2, 256) if d_ == 0 else CS(C_SU2, 256)
                        tri = CS(C_U2, 128) if d_ == 0 else CS(C_L2, 128)
                        ones = CS(C_ONES, 128)
                        rkv = A["rkv"]
                        ld(pv[:, :], RKV[s].t[t0:t0 + 128, :], [RKV[s].b], [pv.b])
                        ld(rkv[:, :], RKV[s].t[t0 + 1:t0 + 129, :], [RKV[s].b], [rkv.b])
                        ld(nx[:, :], RKV[s].t[t0 + 2:t0 + 130, :], [RKV[s].b], [nx.b])
                        ld(A["lw"][:, :], DL[s].t[t0:t0 + 128, d_ * 512:(d_ + 1) * 512], [DL[s].b], [A["lw"].b])
                        ld(A["ic"][:, :], IC[s].t[t0:t0 + 128, d_ * 512:(d_ + 1) * 512], [IC[s].b], [A["ic"].b])
                        tt("pool", pv[:, :], pv[:, :], cwt[:, 0:1536], ALU.mult, [pv.b, cwt.b], [pv.b])
                        tt("pool", rkv[:, :], rkv[:, :], cwt[:, 1536:3072], ALU.mult, [rkv.b, cwt.b], [rkv.b])
                        tt("pool", rkv[:, :], rkv[:, :], pv[:, :], ALU.add, [rkv.b, pv.b], [rkv.b])
                        tt("pool", nx[:, :], nx[:, :], cwt[:, 3072:4608], ALU.mult, [nx.b, cwt.b], [nx.b])
                        tt("pool", rkv[:, :], rkv[:, :], nx[:, :], ALU.add, [rkv.b, nx.b], [rkv.b])
                        r_ = rkv[:, 0:512]
                        k_ = rkv[:, 512:1024]
                        v_ = rkv[:, 1024:1536]
                        kk, kd, bb, lw, ic, sm = A["kk"], A["kd"], A["bb"], A["lw"], A["ic"], A["sm"]
                        e1, e2, e3, e4 = A["e1"], A["e2"], A["e3"], A["e4"]
                        tt("dve", kk[:, :], k_, kkt[:, :], ALU.mult, [rkv.b, kkt.b], [kk.b])
                        act(e1[:, :], kk[:, :], AF.Square, [kk.b], [e1.b])
                        red(sm[:, 0:8], e1[:, :].rearrange("p (h d) -> p h d", d=64), [e1.b], [sm.b])
                        rstd_from(sm[:, 0:8], sm[:, 0:8], 1.0, 1e-12, sm.b)
                        tt("dve", h3(kk[:, :]), h3(kk[:, :]), bc(sm[:, 0:8], 8), ALU.mult, [kk.b, sm.b], [kk.b])
                        stt("dve", e1[:, :], ic[:, :], -1.0, kat[:, :], ALU.add, ALU.mult, [ic.b, kat.b], [e1.b])
                        stt("dve", kd[:, :], e1[:, :], 1.0, k_, ALU.add, ALU.mult, [e1.b, rkv.b], [kd.b])
                        tt("dve", bb[:, :], kk[:, :], ic[:, :], ALU.mult, [kk.b, ic.b], [bb.b])
                        tt("pool", e1[:, :], r_, kd[:, :], ALU.mult, [rkv.b, kd.b], [e1.b])
                        tt("pool", e1[:, :], e1[:, :], rkt[:, :], ALU.mult, [e1.b, rkt.b], [e1.b])
                        red(sm[:, 8:16], e1[:, :].rearrange("p (h d) -> p h d", d=64), [e1.b], [sm.b])
                        ts("dve", sm[:, 8:16], sm[:, 8:16], 0.5, None, ALU.mult, None, [sm.b], [sm.b])
                        tt("dve", h3(A["bo"][:, :]), h3(v_), bc(sm[:, 8:16], 8), ALU.mult, [rkv.b, sm.b], [A["bo"].b])
                        st(BD[s][d_].t[t0:t0 + 128, :], A["bo"][:, :], [A["bo"].b], [BD[s][d_].b])
                        mm(ps(0), tri, lw[:, :], True, True, [cst.b, lw.b], [PB[0]])
                        mm(ps(1), ones, lw[:, :], True, True, [cst.b, lw.b], [PB[1]])
                        for hh in range(8):
                            mm(ps(2, hh, hh + 1, 0, 64), lw[:, hh * 64:(hh + 1) * 64], cst[:, C_ONES:C_ONES + 1], True, True,
                               [cst.b, lw.b], [PB[2]])
                        act(A["gl"][:, :], ps(2, 0, 8, 0, 64), AF.Exp, [PB[2]], [A["gl"].b])
                        act(e1[:, :], ps(0), AF.Exp, [PB[0]], [e1.b])
                        act(e2[:, :], ps(0), AF.Exp, [PB[0]], [e2.b], scale=-1.0)
                        tt("dve", e3[:, :], ps(0), lw[:, :], ALU.subtract, [PB[0], lw.b], [e3.b])
                        act(e3[:, :], e3[:, :], AF.Exp, [e3.b], [e3.b])
                        act(e4[:, :], ps(1), AF.Exp, [PB[1]], [e4.b])
                        tt("dve", e4[:, :], e4[:, :], e2[:, :], ALU.mult, [e4.b, e2.b], [e4.b])
                        tm = A["tm"]
                        stt("dve", tm[:, 0, :], kk[:, :], -1.0, e3[:, :], ALU.mult, ALU.mult, [kk.b, e3.b], [tm.b])
                        tt("dve", tm[:, 1, :], r_, e1[:, :], ALU.mult, [rkv.b, e1.b], [tm.b])
                        tt("pool", tm[:, 2, :], bb[:, :], e2[:, :], ALU.mult, [bb.b, e2.b], [tm.b])
                        tt("pool", tm[:, 3, :], kd[:, :], e2[:, :], ALU.mult, [kd.b, e2.b], [tm.b])
                        tt("dve", A["bh"][:, :], bb[:, :], e4[:, :], ALU.mult, [bb.b, e4.b], [A["bh"].b])
                        tt("pool", A["kh"][:, :], kd[:, :], e4[:, :], ALU.mult, [kd.b, e4.b], [A["kh"].b])
                        cp("act", A["vb"][:, :], v_, [rkv.b], [A["vb"].b])
                        xT = A["xT"]
                        for hh in range(8):
                            bk = 3 + hh // 2
                            for o in range(4):
                                c_ = ((hh % 2) * 4 + o) * 128
                                tr(psb(bk, c_, c_ + 128, 0, 64), tm[:, o, hh * 64:(hh + 1) * 64], [tm.b], [PB[bk]])
                        for pr in range(4):
                            cp("act" if pr % 2 else "dve", xT[:, 2 * pr:2 * pr + 2, :, :].rearrange("p h o t -> p (h o t)"),
                               psb(3 + pr, 0, 1024, 0, 64), [PB[3 + pr]], [xT.b])
                        G, Wm = A["G"], A["Wm"]
                        for pr in range(4):
                            for hh2 in range(2):
                                hh = 2 * pr + hh2
                                mm(ps(0, hh2 * 256, hh2 * 256 + 256), xT[:, hh, 2, :], xT[:, hh, 0:2, :].rearrange("p o t -> p (o t)"),
                                   True, True, [xT.b], [PB[0]])
                                mm(ps(1, hh2 * 256, hh2 * 256 + 256), xT[:, hh, 3, :], xT[:, hh, 0:2, :].rearrange("p o t -> p (o t)"),
                                   True, True, [xT.b], [PB[1]])
                                mm(ps(2, hh2 * 128, hh2 * 128 + 128), xT[:, hh, 0, :], xT[:, hh, 2, :], True, True, [xT.b], [PB[2]])
                            pA = ps(0).rearrange("p (h o t) -> p h o t", h=2, o=2)
                            pB_ = ps(1).rearrange("p (h o t) -> p h o t", h=2, o=2)
                            m2 = lambda c: c.rearrange("p (h t) -> p h t", h=2)
                            hs = slice(2 * pr, 2 * pr + 2)
                            tt("dve", Wm[:, hs, 1, :], pA[:, :, 0, :], m2(cMsu), ALU.mult, [PB[0], cst.b], [Wm.b])
                            tt("dve", G[:, hs, 0, :], pA[:, :, 1, :], m2(cMu), ALU.mult, [PB[0], cst.b], [G.b])
                            tt("dve", G[:, hs, 1, :], pB_[:, :, 0, :], m2(cMsu), ALU.mult, [PB[1], cst.b], [G.b])
                            tt("dve", G[:, hs, 2, :], pB_[:, :, 1, :], m2(cMu), ALU.mult, [PB[1], cst.b], [G.b])
                            tt("dve", Wm[:, hs, 2, :], ps(2, 0, 256).rearrange("p (h t) -> p h t", h=2), m2(cMsl), ALU.mult,
                               [PB[2], cst.b], [Wm.b])
                        for hh in range(8):
                            tt("pool", Wm[:, hh, 0, :], Wm[:, hh, 1, :], cst[:, 0:128], ALU.add, [Wm.b, cst.b], [Wm.b])
                        for rnd in range(1, 8):
                            for hh in range(8):
                                bk = hh
                                if rnd == 1:
                                    mm(ps(bk, 128, 256), Wm[:, hh, 2, :], Wm[:, hh, 1, :], True, True, [Wm.b], [PB[bk]])
                                    mm(ps(bk, 256, 384), Wm[:, hh, 1, :], Wm[:, hh, 2, :], True, True, [Wm.b], [PB[bk]])
                                elif rnd < 7:
                                    mm(ps(bk, 0, 256), Wm[:, hh, 2, :], Wm[:, hh, 0:2, :].rearrange("p o t -> p (o t)"), True, True,
                                       [Wm.b], [PB[bk]])
                                    mm(ps(bk, 256, 384), Wm[:, hh, 1, :], Wm[:, hh, 2, :], True, True, [Wm.b], [PB[bk]])
                                else:
                                    mm(ps(bk, 0, 128), Wm[:, hh, 2, :], Wm[:, hh, 0, :], True, True, [Wm.b], [PB[bk]])
                            pv8 = psum_all[:, :].rearrange("p (h c) -> p h c", c=512)
                            if rnd > 1:
                                tt("dve", Wm[:, :, 0, :], Wm[:, :, 0, :], pv8[:, :, 0:128], ALU.add, list(PB) + [Wm.b], [Wm.b])
                            if rnd < 7:
                                cp("act", Wm[:, :, 1:3, :], pv8[:, :, 128:384].rearrange("p h (o t) -> p h o t", o=2), list(PB), [Wm.b])
                        ST, STb, vb, rhs, ub = A["ST"], A["STb"], A["vb"], A["rhs"], A["ub"]
                        for hh in range(8):
                            c0, c1 = hh * 64, hh * 64 + 64
                            mm(ps(0, c0, c1), xT[:, hh, 0, :], STb[:, hh, :], True, False, [xT.b, STb.b], [PB[0]])
                            mm(ps(0, c0, c1), G[:, hh, 1, :], vb[:, c0:c1], False, True, [G.b, vb.b], [PB[0]])
                        cp("act", rhs[:, :], ps(0), [PB[0]], [rhs.b])
                        for hh in range(8):
                            c0, c1 = hh * 64, hh * 64 + 64
                            mm(ps(1, c0, c1), Wm[:, hh, 0, :], rhs[:, c0:c1], True, True, [Wm.b, rhs.b], [PB[1]])
                        cp("dve", ub[:, :], ps(1), [PB[1]], [ub.b])
                        for hh in range(8):
                            c0, c1 = hh * 64, hh * 64 + 64
                            mm(ps(2, c0, c1), xT[:, hh, 1, :], STb[:, hh, :], True, False, [xT.b, STb.b], [PB[2]])
                            mm(ps(2, c0, c1), G[:, hh, 0, :], ub[:, c0:c1], False, False, [G.b, ub.b], [PB[2]])
                            mm(ps(2, c0, c1), G[:, hh, 2, :], vb[:, c0:c1], False, True, [G.b, vb.b], [PB[2]])
                        cp("act", A["yo"][:, :], ps(2), [PB[2]], [A["yo"].b])
                        st(YD[s][d_].t[t0:t0 + 128, :], A["yo"][:, :], [A["yo"].b], [YD[s][d_].b])
                        for hh in range(8):
                            c0, c1 = hh * 64, hh * 64 + 64
                            mm(ps(3, c0, c1, 0, 64), A["bh"][:, c0:c1], ub[:, c0:c1], True, False, [A["bh"].b, ub.b], [PB[3]])
                            mm(ps(3, c0, c1, 0, 64), A["kh"][:, c0:c1], vb[:, c0:c1], False, True, [A["kh"].b, vb.b], [PB[3]])
                        tt("dve", ST[:, :, :], ST[:, :, :], bc(A["gl"][:, :], 8), ALU.mult, [ST.b, A["gl"].b], [ST.b])
                        tt("dve", ST[:, :, :], ST[:, :, :], h3(ps(3, 0, 512, 0, 64)), ALU.add, [ST.b, PB[3]], [ST.b])
                        cp("act", STb[:, :, :], ST[:, :, :], [ST.b], [STb.b])
                        P.maybe_emit()
                P.emit()

            with ExitStack() as ph:
                def T_(name, shape, dt=F32):
                    return Tl(ph.enter_context(_sbuf(name, list(shape), dt)))
                lng = bload(ph, "lng", [128, 512], W["ln_x_g"].t[l:l + 1, :].partition_broadcast(128))
                lnb = bload(ph, "lnb", [128, 512], W["ln_x_b"].t[l:l + 1, :].partition_broadcast(128))
                y0 = T_("y0", [128, 512])
                y1 = T_("y1", [128, 512])
                b0 = T_("b0", [128, 512])
                b1 = T_("b1", [128, 512])
                gt_ = T_("gt_", [128, 512])
                sq = T_("sq", [128, 512])
                sm = T_("sm2", [128, 16])
                ob = T_("ob", [128, 512], BF16)
                rtt = T_("rtt", [128, 4, 128], BF16)
                for i in range(NT):
                    t0 = i * 128
                    ld(y0[:, :], YD[s][0].t[t0:t0 + 128, :], [YD[s][0].b], [y0.b])
                    ld(y1[:, :], YD[s][1].t[t0:t0 + 128, :], [YD[s][1].b], [y1.b])
                    ld(b0[:, :], BD[s][0].t[t0:t0 + 128, :], [BD[s][0].b], [b0.b])
                    ld(b1[:, :], BD[s][1].t[t0:t0 + 128, :], [BD[s][1].b], [b1.b])
                    ld(gt_[:, :], GG[s].t[t0:t0 + 128, :], [GG[s].b], [gt_.b])
                    tt("dve", y0[:, :], y0[:, :], y1[:, :], ALU.add, [y0.b, y1.b], [y0.b])
                    tt("pool", b0[:, :], b0[:, :], b1[:, :], ALU.add, [b0.b, b1.b], [b0.b])
                    red(sm[:, 0:8], y0[:, :].rearrange("p (h d) -> p h d", d=64), [y0.b], [sm.b])
                    ts("dve", sm[:, 0:8], sm[:, 0:8], -1.0 / 64, None, ALU.mult, None, [sm.b], [sm.b])
                    tt("dve", h3(y0[:, :]), h3(y0[:, :]), bc(sm[:, 0:8], 8), ALU.add, [y0.b, sm.b], [y0.b])
                    act(sq[:, :], y0[:, :], AF.Square, [y0.b], [sq.b])
                    red(sm[:, 8:16], sq[:, :].rearrange("p (h d) -> p h d", d=64), [sq.b], [sm.b])
                    rstd_from(sm[:, 8:16], sm[:, 8:16], 1.0 / 64, 64e-5, sm.b)
                    tt("dve", h3(y0[:, :]), h3(y0[:, :]), bc(sm[:, 8:16], 8), ALU.mult, [y0.b, sm.b], [y0.b])
                    tt("pool", y0[:, :], y0[:, :], lng[:, :], ALU.mult, [y0.b, lng.b], [y0.b])
                    tt("pool", b0[:, :], b0[:, :], lnb[:, :], ALU.add, [b0.b, lnb.b], [b0.b])
                    tt("dve", y0[:, :], y0[:, :], b0[:, :], ALU.add, [y0.b, b0.b], [y0.b])
                    tt("dve", ob[:, :], y0[:, :], gt_[:, :], ALU.mult, [y0.b, gt_.b], [ob.b])
                    for c in range(4):
                        tr(psb(0, c * 128, (c + 1) * 128), ob[:, c * 128:(c + 1) * 128], [ob.b], [PB[0]])
                    cp("act", rtt[:, :, :].rearrange("p c t -> p (c t)"), psb(0, 0, 512), [PB[0]], [rtt.b])
                    st(RT[s].t[i], rtt[:, :, :].rearrange("p c t -> p (c t)"), [rtt.b], [RT[s].b])
                    P.maybe_emit()
                P.emit()

            with ExitStack() as ph:
                def T_(name, shape, dt=F32):
                    return Tl(ph.enter_context(_sbuf(name, list(shape), dt)))
                ktt = T_("ktt", [64, 2, T], BF16)
                vaa = T_("vaa", [128, NT, 130], BF16)
                for c_ in range(0, T, 2048):
                    c1_ = min(T, c_ + 2048)
                    ld(ktt[:, :, c_:c1_], KT[s].t[:, :, c_:c1_], [KT[s].b], [ktt.b])
                vav = VA[s].t.rearrange("(c p) f -> p c f", p=128)
                for c_ in range(0, NT, 8):
                    c1_ = min(NT, c_ + 8)
                    ld(vaa[:, c_:c1_, :], vav[:, c_:c1_, :], [VA[s].b], [vaa.b])
                qt2 = [T_("qt2_%d" % i, [64, 4, 128], BF16) for i in range(2)]
                pt = [T_("pt%d" % i, [128, 1024], BF16) for i in range(3)]
                rc = T_("rc", [128, 512])
                osb = T_("osb", [64, 512])
                aob = T_("aob", [64, 4, 128], BF16)
                onesr = T_("onesr", [128, 64])
                P.op("dve", lambda h: h.memset(onesr[:, :], 1.0), [], [onesr.b])
                it = 0
                gi = 0
                NG = NT // 2 if NT >= 2 else 1
                GW = 2 if NT >= 2 else 1
                for qb in range(NT):
                    for kv in range(2):
                        q_ = qt2[it % 2]
                        ob_ = 6 + it % 2
                        it += 1
                        ld(q_[:, :, :].rearrange("p h t -> p (h t)"), QT[s].t[qb, :, kv * 512:(kv + 1) * 512], [QT[s].b], [q_.b])
                        for g in range(NG):
                            sbk = (gi % 3) * 2
                            p_ = pt[gi % 3]
                            gi += 1
                            for j in range(GW):
                                kc = g * GW + j
                                mm(ps(sbk + j), ktt[:, kv, kc * 128:(kc + 1) * 128], q_[:, :, :].rearrange("p h t -> p (h t)"), True, True,
                                   [ktt.b, q_.b], [PB[sbk + j]])
                            act(p_[:, 0:GW * 512], psum_all[:, sbk * 512:(sbk + GW) * 512], AF.Exp, [PB[sbk], PB[sbk + 1]], [p_.b],
                                scale=0.125)
                            for j in range(GW):
                                kc = g * GW + j
                                mm(ps(ob_, 0, 512, 0, 65), vaa[:, kc, kv * 65:(kv + 1) * 65], p_[:, j * 512:(j + 1) * 512],
                                   kc == 0, kc == NT - 1, [vaa.b, p_.b], [PB[ob_]])
                        P.op("dve", lambda h, ob_=ob_: h.reciprocal(out=rc[64:65, :], in_=ps(ob_, 0, 512, 64, 65)), [PB[ob_]], [rc.b])
                        cp("act", osb[:, :], ps(ob_, 0, 512, 0, 64), [PB[ob_]], [osb.b])
                        mm(ps(ob_, 0, 512, 0, 64), onesr[64:65, :], rc[64:65, :], True, True, [onesr.b, rc.b, osb.b], [PB[ob_]])
                        tt("dve", aob[:, :, :].rearrange("p h t -> p (h t)"), osb[:, :], ps(ob_, 0, 512, 0, 64), ALU.mult, [osb.b, PB[ob_]], [aob.b])
                        st(AT[s].t[qb, :, kv * 512:(kv + 1) * 512], aob[:, :, :].rearrange("p h t -> p (h t)"), [aob.b], [AT[s].b])
                        P.maybe_emit()
                P.emit()

        with ExitStack() as ph:
            def T_(name, shape, dt=F32):
                return Tl(ph.enter_context(_sbuf(name, list(shape), dt)))
            wor = T_("wor", [128, 4, D], BF16)
            woa = T_("woa", [64, 8, D], BF16)
            stage = [T_("stgd%d" % i, [128, 2048]) for i in range(2)]
            wv = W["w_out"].t[l, 0:512, :].rearrange("(k p) n -> p k n", p=128)
            load_bf16(lambda k, c0, w: (wor[:, k, c0:c0 + w], wor.b, 128), lambda k, c0, w: wv[:, k, c0:c0 + w], 4, D, stage)
            wv2 = W["w_out"].t[l, 512:1024, :].rearrange("(h p) n -> p h n", p=64)
            load_bf16(lambda k, c0, w: (woa[:, k, c0:c0 + w], woa.b, 64), lambda k, c0, w: wv2[:, k, c0:c0 + w], 8, D, stage)
            gateM = T_("gateM", [128, D])
            gainF = T_("gainF", [128, D])
            shiftF = T_("shiftF", [128, D])
            xt = [T_("xd%d" % i, [128, D]) for i in range(2)]
            rtt = T_("rtd", [128, 4, 128], BF16)
            att = T_("atd", [64, 8, 128], BF16)
            x1 = T_("x1", [128, D])
            junk = T_("junkd", [128, D])
            st1 = T_("st1d", [128, 4])
            hb = T_("hbd", [128, D], BF16)
            hT = T_("hTd", [128, 8, 128], BF16)
            for s in range(NS):
                T = seqs[s]
                src = xin[s] if l == 0 else XS[s]
                bcast_table(gateM, s, 2)
                bcast_table(gainF, s, 4)
                bcast_table(shiftF, s, 3)
                for i in range(T // 128):
                    t0 = i * 128
                    x_ = xt[i % 2]
                    ld(x_[:, :], src.t[t0:t0 + 128, :], [src.b], [x_.b])
                    ld(rtt[:, :, :].rearrange("p c t -> p (c t)"), RT[s].t[i], [RT[s].b], [rtt.b])
                    ld(att[:, :, :].rearrange("p h t -> p (h t)"), AT[s].t[i], [AT[s].b], [att.b])
                    for hf in range(2):
                        for k in range(4):
                            mm(ps(hf), rtt[:, k, :], wor[:, k, hf * 512:(hf + 1) * 512], k == 0, False, [rtt.b, wor.b], [PB[hf]])
                        for k in range(8):
                            mm(ps(hf), att[:, k, :], woa[:, k, hf * 512:(hf + 1) * 512], False, k == 7, [att.b, woa.b], [PB[hf]])
                    tt("dve", junk[:, :], psum_all[:, 0:1024], gateM[:, :], ALU.mult, [PB[0], PB[1], gateM.b], [junk.b])
                    tt("dve", x1[:, :], junk[:, :], x_[:, :], ALU.add, [junk.b, x_.b], [x1.b])
                    st(X1[s].t[t0:t0 + 128, :], x1[:, :], [x1.b], [X1[s].b])
                    P.op("dve", lambda h: h.memset(st1[:, 0:1], 0.0), [], [st1.b])
                    act(junk[:, :], x1[:, :], AF.Square, [x1.b], [junk.b, st1.b], accum_out=st1[:, 0:1])
                    rstd_from(st1[:, 0:1], st1[:, 0:1], 1.0 / D, 1e-6, st1.b)
                    stt("dve", junk[:, :], x1[:, :], st1[:, 0:1], gainF[:, :], ALU.mult, ALU.mult, [x1.b, st1.b, gainF.b], [junk.b])
                    tt("dve", hb[:, :], junk[:, :], shiftF[:, :], ALU.add, [junk.b, shiftF.b], [hb.b])
                    for k in range(8):
                        tr(psb(2, k * 128, (k + 1) * 128), hb[:, k * 128:(k + 1) * 128], [hb.b], [PB[2]])
                    cp("act", hT[:, :, :].rearrange("p k t -> p (k t)"), psb(2), [PB[2]], [hT.b])
                    st(H2T[s].t[i], hT[:, :, :].rearrange("p k t -> p (k t)"), [hT.b], [H2T[s].b])
                    P.maybe_emit()
            P.emit()

        with ExitStack() as ph:
            def T_(name, shape, dt=F32):
                return Tl(ph.enter_context(_sbuf(name, list(shape), dt)))
            wfi = T_("wfi", [128, 8, 2 * DFF], BF16)
            wfo = T_("wfo", [128, 22, D], BF16)
            with ExitStack() as ph2:
                stage = [Tl(ph2.enter_context(_sbuf("stgf%d" % i, [128, 2048], F32))) for i in range(2)]
                wv = W["w_ffn_in"].t[l].rearrange("(k p) n -> p k n", p=128)
                load_bf16(lambda k, c0, w: (wfi[:, k, c0:c0 + w], wfi.b, 128), lambda k, c0, w: wv[:, k, c0:c0 + w], 8, 2 * DFF, stage)
                wv2 = W["w_ffn_out"].t[l].rearrange("(k p) n -> p k n", p=128)
                load_bf16(lambda k, c0, w: (wfo[:, k, c0:c0 + w], wfo.b, 128), lambda k, c0, w: wv2[:, k, c0:c0 + w], 22, D, stage)
                P.emit()
            gateF = T_("gateF", [128, D])
            h2 = T_("h2", [128, 8, 512], BF16)
            actT = T_("actT", [128, 22, 512], BF16)
            sg = [T_("sg%d" % i, [128, 512]) for i in range(2)]
            xt = [T_("xf%d" % i, [128, D]) for i in range(2)]
            ti = 0
            for s in range(NS):
                T = seqs[s]
                dst = XS[s] if l < depth - 1 else yout[s]
                bcast_table(gateF, s, 5)
                GT = 512 if T % 512 == 0 else (256 if T % 256 == 0 else 128)
                for gidx in range(T // GT):
                    g0 = gidx * GT
                    for j_ in range(GT // 128):
                        ld(h2[:, :, j_ * 128:(j_ + 1) * 128], H2T[s].t[g0 // 128 + j_].rearrange("p (k t) -> p k t", k=8),
                           [H2T[s].b], [h2.b])
                    for f in range(22):
                        ba, bu = 2 * (f % 2), 2 * (f % 2) + 1
                        for k in range(8):
                            mm(ps(ba, 0, GT), wfi[:, k, f * 128:(f + 1) * 128], h2[:, k, 0:GT], k == 0, k == 7, [wfi.b, h2.b], [PB[ba]])
                        for k in range(8):
                            mm(ps(bu, 0, GT), wfi[:, k, DFF + f * 128:DFF + (f + 1) * 128], h2[:, k, 0:GT], k == 0, k == 7,
                               [wfi.b, h2.b], [PB[bu]])
                        s_ = sg[f % 2]
                        act(s_[:, 0:GT], ps(ba, 0, GT), AF.Silu, [PB[ba]], [s_.b])
                        tt("dve", actT[:, f, 0:GT], s_[:, 0:GT], ps(bu, 0, GT), ALU.mult, [s_.b, PB[bu]], [actT.b])
                    for tsub in range(GT // 128):
                        t0 = g0 + tsub * 128
                        x_ = xt[ti % 2]
                        ti += 1
                        ld(x_[:, :], X1[s].t[t0:t0 + 128, :], [X1[s].b], [x_.b])
                        for hf in range(2):
                            bk = 4 + hf + 2 * (ti % 2)
                            for f in range(22):
                                mm(ps(bk), actT[:, f, tsub * 128:(tsub + 1) * 128], wfo[:, f, hf * 512:(hf + 1) * 512], f == 0, f == 21,
                                   [actT.b, wfo.b], [PB[bk]])
                            s2_ = sg[hf]
                            tt("dve", s2_[:, 0:512], ps(bk), gateF[:, hf * 512:(hf + 1) * 512], ALU.mult,
                               [PB[bk], gateF.b], [s2_.b])
                            tt("pool", x_[:, hf * 512:(hf + 1) * 512], x_[:, hf * 512:(hf + 1) * 512], s2_[:, 0:512], ALU.add,
                               [s2_.b, x_.b], [x_.b])
                        st(dst.t[t0:t0 + 128, :], x_[:, :], [x_.b], [dst.b])
                    P.maybe_emit()
            P.emit()

    P.wait_all("sp", [y.b for y in yout])
    P.emit()
    es.close()
    return nc


_CACHE = {}


def _run(seqs, depth, per_core_inputs, debug=False):
    key = (tuple(seqs), depth, debug)
    if key not in _CACHE:
        _CACHE[key] = build(seqs, depth, debug)
    nc = _CACHE[key]
    res = run_bass_kernel_spmd(nc, per_core_inputs, core_ids=list(range(len(per_core_inputs))))
    return res


def kernel(**inputs):
    f32 = lambda a: np.ascontiguousarray(np.asarray(a, dtype=np.float32))
    xp = f32(inputs["x_prompt"])
    xs = f32(inputs["x_sample"])
    cp_ = f32(inputs["c_prompt"])
    cs = f32(inputs["c_sample"])
    TP, TS = xp.shape[1], xs.shape[1]
    nper = xs.shape[0] // 8
    depth = inputs["w_in"].shape[0]
    wnames = ["ada_w", "ada_b", "norm_mix_g", "norm_ffn_g", "w_in", "conv_w", "decay_w0", "decay_up", "iclr_a0", "iclr_up",
              "gate_up", "k_k", "k_a", "r_k", "ln_x_g", "ln_x_b", "q_norm_g", "k_norm_g", "w_out", "w_ffn_in", "w_ffn_out"]
    DUM = 128
    zx = np.zeros((DUM, D), np.float32)
    zc = np.zeros((D,), np.float32)
    consts = {}
    for T_ in (TP, TS):
        cst, cosq, sinq = host_consts(max(T_, DUM))
        consts[T_] = dict(cst=cst, cosq=cosq, sinq=sinq)
    cur_p = xp[0]
    cur_s = [xs[i] for i in range(xs.shape[0])]
    seg = TP // 8
    for l0 in range(depth):
        wl = {k: f32(inputs[k][l0:l0 + 1]) for k in wnames}
        maps = []
        for c in range(8):
            m = dict(wl)
            m.update(consts[TP])
            m["x0"] = cur_p
            m["x1"] = zx
            m["x2"] = zx
            m["cvec"] = np.stack([cp_[0], zc, zc], 0)
            maps.append(m)
        res = _run([TP, DUM, DUM], 1, maps)
        nxt = np.empty_like(cur_p)
        for c in range(8):
            nxt[c * seg:(c + 1) * seg] = res.results[c]["y0"][c * seg:(c + 1) * seg]
        new_s = list(cur_s)
        for j in range(nper):
            maps = []
            for c in range(8):
                m = dict(wl)
                m.update(consts[TS])
                m["x0"] = cur_s[c * nper + j]
                m["x1"] = zx
                m["x2"] = zx
                m["cvec"] = np.stack([cs[c * nper + j], zc, zc], 0)
                maps.append(m)
            res = _run([TS, DUM, DUM], 1, maps)
            for c in range(8):
                new_s[c * nper + j] = res.results[c]["y0"]
        cur_p = nxt
        cur_s = new_s
    yp = cur_p[None].astype(np.float32)
    ys = np.stack(cur_s, 0).astype(np.float32)
    return (yp, ys)
```
